# Optimizing a Trainium2 kernel written in Bass

```python
import math
import jax, jax.numpy as jnp
from jax import lax
import numpy as np

D_MODEL = 2048
BATCH = 8
SEQ = 2048
DEPTH = 2
DEC_BATCH = 128
DEC_SEQ = 8
PAST_LEN = 2048
PAGE_SIZE = 128

N_MIXERS = 2
N_A_LAYERS = (DEPTH + N_MIXERS - 1) // N_MIXERS
N_B_LAYERS = DEPTH // N_MIXERS
CHUNK = 128
A_WIDTH = D_MODEL
A_GROUPS = 16
A_GROUP_DIM = A_WIDTH // A_GROUPS
N_HEADS = 16
HEAD_DIM = D_MODEL // N_HEADS
N_KV_HEADS = 4
Q_PER_KV = N_HEADS // N_KV_HEADS
IDX_HEADS = 16
IDX_DIM = 64
TOPK_MAX = 256
Q_BLOCK = 128
ROPE_THETA = 10000.0
IDX_W_SCALE = (IDX_HEADS * IDX_DIM) ** -0.5
B_Q = N_HEADS * HEAD_DIM
B_KV = N_KV_HEADS * HEAD_DIM
B_IQ = IDX_HEADS * IDX_DIM
B_PROJ = B_Q + 2 * B_KV + B_IQ + IDX_DIM + IDX_HEADS
B_SPLITS = [B_Q, B_Q + B_KV, B_Q + 2 * B_KV, B_Q + 2 * B_KV + B_IQ, B_Q + 2 * B_KV + B_IQ + IDX_DIM]
D_FF = 5632
CONV_WIDTH = 3
EPS = 1e-6

kernel_name = "hybrid_chunkgmlp_dsa_convffn_step"


def _rmsnorm(x, g):
    xf = x.astype(jnp.float32)
    y = xf * lax.rsqrt(jnp.mean(xf * xf, axis=-1, keepdims=True) + EPS)
    return (y * g.astype(jnp.float32)).astype(x.dtype)


def _rope(x, pos):
    d = x.shape[-1]
    half = d // 2
    inv = ROPE_THETA ** (-jnp.arange(half, dtype=jnp.float32) * (2.0 / d))
    ang = pos.astype(jnp.float32)[:, None] * inv[None, :]
    cos = jnp.cos(ang)[:, None, :]
    sin = jnp.sin(ang)[:, None, :]
    xf = x.astype(jnp.float32)
    x1, x2 = xf[..., :half], xf[..., half:]
    return jnp.concatenate([x1 * cos - x2 * sin, x2 * cos + x1 * sin], axis=-1).astype(x.dtype)


def _chunk_mlp(h, w_in, v_gain, w_s, b_s, w_out):
    bsz, t, _ = h.shape
    c = min(t, CHUNK)
    n = t // c
    z = jax.nn.gelu(h @ w_in)
    u, v = z[..., :A_WIDTH], z[..., A_WIDTH:]
    v = _rmsnorm(v, v_gain)
    causal = jnp.tril(jnp.ones((c, c), dtype=bool))
    ws = jnp.where(causal[None], w_s[:, :c, :c], 0).astype(v.dtype)
    vg = v.reshape(bsz, n, c, A_GROUPS, A_GROUP_DIM)
    s = jnp.einsum('gts,bnsgd->bntgd', ws, vg) + b_s[:, :c].T[:, :, None]
    s = s.reshape(bsz, t, A_WIDTH)
    return (u * s) @ w_out, v


def _dsa_project(h, pos, w_in, q_gain, k_gain):
    bsz, t, _ = h.shape
    p = h @ w_in
    q, k, v, iq, ik, iw = jnp.split(p, B_SPLITS, axis=-1)
    q = _rope(_rmsnorm(q.reshape(bsz, t, N_HEADS, HEAD_DIM), q_gain), pos)
    k = _rope(_rmsnorm(k.reshape(bsz, t, N_KV_HEADS, HEAD_DIM), k_gain), pos)
    v = v.reshape(bsz, t, N_KV_HEADS, HEAD_DIM)
    iq = _rope(iq.reshape(bsz, t, IDX_HEADS, IDX_DIM), pos)
    ik = _rope(ik[:, :, None, :], pos)[:, :, 0, :]
    iw = iw * IDX_W_SCALE
    return q, k, v, iq, ik, iw


def _index_select(iq, iw, ik, qpos, kpos, topk):
    logits = jnp.einsum('bqhd,bsd->bqsh', iq, ik, preferred_element_type=jnp.float32)
    score = jnp.einsum('bqsh,bqh->bqs', jax.nn.relu(logits), iw.astype(jnp.float32))
    admissible = kpos[None, :] <= qpos[:, None]
    score = jnp.where(admissible[None], score, -jnp.inf)
    _, sel = lax.top_k(score, topk)
    valid = sel <= qpos[None, :, None]
    return sel, valid


def _sparse_attend(q, k_sel, v_sel, valid):
    bsz, t = q.shape[:2]
    qg = q.reshape(bsz, t, N_KV_HEADS, Q_PER_KV, HEAD_DIM)
    s = jnp.einsum('btkgd,btskd->btkgs', qg, k_sel, preferred_element_type=jnp.float32) * (HEAD_DIM ** -0.5)
    s = jnp.where(valid[:, :, None, None, :], s, -jnp.inf)
    p = jax.nn.softmax(s, axis=-1).astype(v_sel.dtype)
    o = jnp.einsum('btkgs,btskd->btkgd', p, v_sel)
    return o.reshape(bsz, t, N_HEADS * HEAD_DIM)


def _dsa_prompt(h, w_in, q_gain, k_gain, w_o):
    bsz, s_len, _ = h.shape
    pos = jnp.arange(s_len, dtype=jnp.int32)
    q, k, v, iq, ik, iw = _dsa_project(h, pos, w_in, q_gain, k_gain)
    topk = min(TOPK_MAX, s_len // 4)
    bidx = jnp.arange(bsz)[:, None, None]

    def block(start):
        qpos = start + jnp.arange(Q_BLOCK, dtype=jnp.int32)
        q_b = lax.dynamic_slice_in_dim(q, start, Q_BLOCK, axis=1)
        iq_b = lax.dynamic_slice_in_dim(iq, start, Q_BLOCK, axis=1)
        iw_b = lax.dynamic_slice_in_dim(iw, start, Q_BLOCK, axis=1)
        sel, valid = _index_select(iq_b, iw_b, ik, qpos, pos, topk)
        return _sparse_attend(q_b, k[bidx, sel], v[bidx, sel], valid)

    starts = jnp.arange(0, s_len, Q_BLOCK, dtype=jnp.int32)
    o = lax.map(block, starts)
    o = jnp.moveaxis(o, 0, 1).reshape(bsz, s_len, N_HEADS * HEAD_DIM)
    return o @ w_o, k, v, ik


def _gather_rows(pool, layer, page_table, new, sel, past):
    bidx = jnp.arange(sel.shape[0])[:, None, None]
    ps = jnp.minimum(sel, past - 1)
    old = pool[layer, page_table[bidx, ps // PAGE_SIZE], ps % PAGE_SIZE]
    cur = new[bidx, jnp.clip(sel - past, 0, new.shape[1] - 1)]
    return jnp.where((sel < past)[..., None, None], old, cur)


def _dsa_sample(h, layer, cache_k, cache_v, cache_idx_k, page_table, w_in, q_gain, k_gain, w_o):
    bsz, t, _ = h.shape
    past = page_table.shape[1] * PAGE_SIZE
    pos = past + jnp.arange(t, dtype=jnp.int32)
    q, k, v, iq, ik, iw = _dsa_project(h, pos, w_in, q_gain, k_gain)
    ik_past = cache_idx_k[layer, page_table].reshape(bsz, past, IDX_DIM)
    ik_all = jnp.concatenate([ik_past, ik.astype(ik_past.dtype)], axis=1)
    kpos = jnp.arange(past + t, dtype=jnp.int32)
    topk = min(TOPK_MAX, (past + t) // 4)
    sel, valid = _index_select(iq, iw, ik_all, pos, kpos, topk)
    k_sel = _gather_rows(cache_k, layer, page_table, k, sel, past)
    v_sel = _gather_rows(cache_v, layer, page_table, v, sel, past)
    o = _sparse_attend(q, k_sel, v_sel, valid)
    return o @ w_o, k, v, ik


def _conv_ffn(h, conv_buf, w_in, conv_w, conv_b, w_out):
    t = h.shape[1]
    a = h @ w_in
    ap = jnp.concatenate([conv_buf.astype(a.dtype), a], axis=1)
    c = conv_b
    for j in range(CONV_WIDTH):
        c = c + conv_w[j] * ap[:, j:j + t]
    g, up = c[..., :D_FF], c[..., D_FF:]
    return (jax.nn.silu(g) * up) @ w_out, ap[:, t:]


def setup_inputs(seed: int = 0) -> dict:
    key = jax.random.key(seed)
    ks = jax.random.split(key, 32)
    n_pages = PAST_LEN // PAGE_SIZE
    n_used = DEC_BATCH * n_pages
    n_pool = n_used + (n_used + 3) // 4

    def nrm(k, shape, scale=1.0):
        return jax.random.normal(k, shape, jnp.float32) * scale

    def gain(k, shape):
        return 1.0 + 0.02 * jax.random.normal(k, shape, jnp.float32)

    page_table = jax.random.permutation(ks[7], n_pool)[:n_used].reshape(DEC_BATCH, n_pages).astype(jnp.int32)
    return {
        'x_prompt': nrm(ks[0], (BATCH, SEQ, D_MODEL)),
        'x_sample': nrm(ks[1], (DEC_BATCH, DEC_SEQ, D_MODEL)),
        'cache_k': nrm(ks[2], (N_B_LAYERS, n_pool, PAGE_SIZE, N_KV_HEADS, HEAD_DIM)),
        'cache_v': nrm(ks[3], (N_B_LAYERS, n_pool, PAGE_SIZE, N_KV_HEADS, HEAD_DIM)),
        'cache_idx_k': nrm(ks[4], (N_B_LAYERS, n_pool, PAGE_SIZE, IDX_DIM)),
        'state_ffn_conv': nrm(ks[5], (DEPTH, DEC_BATCH, CONV_WIDTH - 1, 2 * D_FF)),
        'page_table': page_table,
        'a_norm': gain(ks[8], (N_A_LAYERS, D_MODEL)),
        'a_w_in': nrm(ks[9], (N_A_LAYERS, D_MODEL, 2 * A_WIDTH), D_MODEL ** -0.5),
        'a_v_norm': gain(ks[10], (N_A_LAYERS, A_WIDTH)),
        'a_w_s': nrm(ks[11], (N_A_LAYERS, A_GROUPS, CHUNK, CHUNK), CHUNK ** -0.5),
        'a_b_s': gain(ks[12], (N_A_LAYERS, A_GROUPS, CHUNK)),
        'a_w_out': nrm(ks[13], (N_A_LAYERS, A_WIDTH, D_MODEL), A_WIDTH ** -0.5),
        'b_norm': gain(ks[14], (N_B_LAYERS, D_MODEL)),
        'b_w_in': nrm(ks[15], (N_B_LAYERS, D_MODEL, B_PROJ), D_MODEL ** -0.5),
        'b_q_norm': gain(ks[16], (N_B_LAYERS, HEAD_DIM)),
        'b_k_norm': gain(ks[17], (N_B_LAYERS, HEAD_DIM)),
        'b_w_o': nrm(ks[18], (N_B_LAYERS, N_HEADS * HEAD_DIM, D_MODEL), (N_HEADS * HEAD_DIM) ** -0.5),
        'f_norm': gain(ks[19], (DEPTH, D_MODEL)),
        'f_w_in': nrm(ks[20], (DEPTH, D_MODEL, 2 * D_FF), D_MODEL ** -0.5),
        'f_conv_w': nrm(ks[21], (DEPTH, CONV_WIDTH, 2 * D_FF), CONV_WIDTH ** -0.5),
        'f_conv_b': nrm(ks[22], (DEPTH, 2 * D_FF), 0.01),
        'f_w_out': nrm(ks[23], (DEPTH, D_FF, D_MODEL), D_FF ** -0.5),
    }


def reference(x_prompt, x_sample, cache_k, cache_v, cache_idx_k, state_ffn_conv, page_table,
              a_norm, a_w_in, a_v_norm, a_w_s, a_b_s, a_w_out,
              b_norm, b_w_in, b_q_norm, b_k_norm, b_w_o,
              f_norm, f_w_in, f_conv_w, f_conv_b, f_w_out):
    xp, xs = x_prompt, x_sample
    bsz, seq = xp.shape[:2]
    n_seq_pages = seq // PAGE_SIZE
    k_p, v_p, ik_p, k_s, v_s, ik_s, chunk_v_s, conv_p, conv_s = [], [], [], [], [], [], [], [], []
    for layer in range(DEPTH):
        j = layer // N_MIXERS
        if layer % N_MIXERS == 0:
            op, _ = _chunk_mlp(_rmsnorm(xp, a_norm[j]), a_w_in[j], a_v_norm[j], a_w_s[j], a_b_s[j], a_w_out[j])
            os_, vrows = _chunk_mlp(_rmsnorm(xs, a_norm[j]), a_w_in[j], a_v_norm[j], a_w_s[j], a_b_s[j], a_w_out[j])
            chunk_v_s.append(vrows)
        else:
            op, kp, vp, ikp = _dsa_prompt(_rmsnorm(xp, b_norm[j]), b_w_in[j], b_q_norm[j], b_k_norm[j], b_w_o[j])
            os_, ksn, vsn, iks = _dsa_sample(_rmsnorm(xs, b_norm[j]), j, cache_k, cache_v, cache_idx_k, page_table,
                                             b_w_in[j], b_q_norm[j], b_k_norm[j], b_w_o[j])
            k_p.append(kp.reshape(bsz, n_seq_pages, PAGE_SIZE, N_KV_HEADS, HEAD_DIM))
            v_p.append(vp.reshape(bsz, n_seq_pages, PAGE_SIZE, N_KV_HEADS, HEAD_DIM))
            ik_p.append(ikp.reshape(bsz, n_seq_pages, PAGE_SIZE, IDX_DIM))
            k_s.append(ksn)
            v_s.append(vsn)
            ik_s.append(iks)
        xp = xp + op
        xs = xs + os_
        zeros_buf = jnp.zeros((bsz, CONV_WIDTH - 1, 2 * D_FF), xp.dtype)
        fp, cp = _conv_ffn(_rmsnorm(xp, f_norm[layer]), zeros_buf, f_w_in[layer], f_conv_w[layer], f_conv_b[layer], f_w_out[layer])
        fs, cs = _conv_ffn(_rmsnorm(xs, f_norm[layer]), state_ffn_conv[layer], f_w_in[layer], f_conv_w[layer], f_conv_b[layer], f_w_out[layer])
        xp = xp + fp
        xs = xs + fs
        conv_p.append(cp)
        conv_s.append(cs)
    return (xp, xs, jnp.stack(k_p), jnp.stack(v_p), jnp.stack(ik_p), jnp.stack(k_s), jnp.stack(v_s), jnp.stack(ik_s), jnp.stack(chunk_v_s), jnp.stack(conv_p), jnp.stack(conv_s))
```

```python
import contextlib
import math
import numpy as np
import ml_dtypes
import concourse.bass as bass
import concourse.mybir as mybir
from concourse.bass_utils import run_bass_kernel_spmd

F32 = mybir.dt.float32
BF16 = mybir.dt.bfloat16
I32 = mybir.dt.int32
AF = mybir.ActivationFunctionType
ALU = mybir.AluOpType
AX = mybir.AxisListType

D = 2048
SEQ = 2048
DFF = 5632
NFC = 44
PAGE = 128
NPG = 16
NSEQ = 16
EPS = 1e-6
IDX_W_SCALE = (16 * 64) ** -0.5
ATT_SCALE = 128 ** -0.5
NEG = -1.0e30


class Buf:
    __slots__ = ("name", "last_w", "readers", "sem", "cnt", "last_dma")

    def __init__(self, name):
        self.name = name
        self.last_w = None
        self.readers = []
        self.sem = None
        self.cnt = 0
        self.last_dma = None


class Op:
    __slots__ = ("eng", "fn", "deps", "sig", "ms", "dma", "ninst")

    def __init__(self, eng, fn, dma):
        self.eng = eng
        self.fn = fn
        self.deps = []
        self.sig = False
        self.ms = None
        self.dma = dma
        self.ninst = 1


class Sched:
    ENGS = ("pe", "act", "dve", "pool", "sp")

    def __init__(self, nc):
        self.nc = nc
        self.ops = []

    def add(self, eng, fn, reads=(), writes=(), dma=None, ndma=1):
        op = Op(eng, fn, dma)
        raw = set()
        war = set()
        for b in reads:
            if b.last_w is not None:
                raw.add(b.last_w)
            b.readers.append(op)
        for b in writes:
            if b.last_w is not None:
                war.add(b.last_w)
            for r in b.readers:
                if r is not op:
                    war.add(r)
            b.readers = []
            b.last_w = op
        if dma is not None:
            if dma.last_dma is not None:
                raw.add(dma.last_dma)
            dma.last_dma = op
            dma.cnt += 16 * ndma
            op.ms = dma.cnt
            op.ninst = ndma
        raw.discard(op)
        war.discard(op)
        keep = []
        for d in raw | war:
            if d.dma is None and op.dma is None and d.eng == eng:
                if eng == "pe" or d not in raw:
                    continue
            keep.append(d)
        op.deps = keep
        for d in keep:
            d.sig = True
        self.ops.append(op)
        return op

    def emit(self, final_wait_bufs=()):
        nc = self.nc
        with contextlib.ExitStack() as st:
            esem = {}
            for e in self.ENGS:
                esem[e] = st.enter_context(nc.semaphore("s_" + e))
            cnt = {e: 0 for e in self.ENGS}
            for op in self.ops:
                if op.dma is None and op.sig:
                    cnt[op.eng] += 1
                    op.ms = cnt[op.eng]
                if op.dma is not None and op.dma.sem is None:
                    op.dma.sem = st.enter_context(nc.semaphore("d_" + op.dma.name))
            per = {e: [o for o in self.ops if o.eng == e] for e in self.ENGS}
            block = st.enter_context(nc.Block())

            def run(e, eng):
                seen = {}
                for op in per[e]:
                    need = {}
                    for d in op.deps:
                        if d.dma is not None:
                            s, v = d.dma.sem, d.ms
                        else:
                            s, v = esem[d.eng], d.ms
                        k = id(s)
                        if seen.get(k, 0) >= v:
                            continue
                        if k not in need or need[k][1] < v:
                            need[k] = (s, v)
                    for k, (s, v) in need.items():
                        eng.wait_ge(s, v)
                        seen[k] = v
                    r = op.fn(eng)
                    if op.dma is not None:
                        rl = r if isinstance(r, (list, tuple)) else [r]
                        assert len(rl) == op.ninst, (len(rl), op.ninst)
                        for i in rl:
                            i.then_inc(op.dma.sem, 16)
                    elif op.sig:
                        assert r is not None
                        r.then_inc(esem[e], 1)
                if e == "sp":
                    for b in final_wait_bufs:
                        if b.sem is not None and seen.get(id(b.sem), 0) < b.cnt:
                            eng.wait_ge(b.sem, b.cnt)

            @block.tensor
            def _(eng):
                run("pe", eng)

            @block.scalar
            def _(eng):
                run("act", eng)

            @block.vector
            def _(eng):
                run("dve", eng)

            @block.gpsimd
            def _(eng):
                run("pool", eng)

            @block.sync
            def _(eng):
                run("sp", eng)


def build_program(NP):
    nc = bass.Bass("TRN2", target_bir_lowering=False)
    S = Sched(nc)
    ts = contextlib.ExitStack()

    def din(name, shape, dt=F32):
        return nc.dram_tensor(name, list(shape), dt, kind="ExternalInput").ap()

    def dout(name, shape):
        return nc.dram_tensor(name, list(shape), F32, kind="ExternalOutput").ap()

    xp = din("xp", [SEQ, D])
    xs = din("xs", [128, D])
    ck = din("ck", [NP * PAGE, 512])
    cv = din("cv", [NP * PAGE, 512])
    cik = din("cik", [NP * PAGE, 64])
    conv_s = din("conv_s", [2, 32, 2 * DFF])
    ptab = din("ptab", [1, NSEQ * NPG], I32)
    a_w_in = din("a_w_in", [D, 2 * D])
    a_w_out = din("a_w_out", [D, D])
    b_w_in = din("b_w_in", [D, 4176])
    b_w_o = din("b_w_o", [D, D])
    f_w_in = din("f_w_in", [2, D, 2 * DFF])
    f_w_out = din("f_w_out", [2, DFF, D])
    gains_d = din("gains", [128, 64])
    vgT_d = din("vgT", [128, 16])
    vg_row = din("vg_row", [1, D])
    wsT_p = din("wsT_p", [128, 16 * 128])
    wsT_s = din("wsT_s", [128, 16 * 128])
    bs_p = din("bs_p", [1, 16 * 128])
    bs_s = din("bs_s", [1, 16 * 128])
    qg_row = din("qg_row", [1, 128])
    kg_row = din("kg_row", [1, 128])
    cw_d = din("cw", [128, 2 * 88 * 4])
    ropeqk = din("ropeqk", [SEQ + 128, 128])
    ropei = din("ropei", [SEQ + 128, 64])
    c_newmask = din("c_newmask", [128, 128])
    c_ee = din("c_ee", [128, 248])

    y_p = dout("y_p", [SEQ, D])
    y_s = dout("y_s", [128, D])
    nk_p = dout("nk_p", [SEQ, 512])
    nv_p = dout("nv_p", [SEQ, 512])
    nik_p = dout("nik_p", [SEQ, 64])
    nk_s = dout("nk_s", [128, 512])
    nv_s = dout("nv_s", [128, 512])
    nik_s = dout("nik_s", [128, 64])
    ncv_s = dout("ncv_s", [128, D])
    nconv_p = dout("nconv_p", [2, 2, 2 * DFF])
    nconv_s = dout("nconv_s", [2, 32, 2 * DFF])

    def sb(name, shape, dt):
        return ts.enter_context(nc.sbuf_tensor(name, list(shape), dt))

    xT = sb("xT", [128, 16, 512], F32)
    hT = sb("hT", [128, 16, 512], BF16)
    actT = sb("actT", [128, 16, 512], BF16)
    NSLOT = 4
    wring = [sb("wr%d" % i, [128, 4096], BF16) for i in range(NSLOT)]
    kT = sb("kT", [128, 4, SEQ], BF16)
    vtok = sb("vtok", [128, 16, 512], BF16)
    ikT = sb("ikT", [128, SEQ], BF16)
    identf = sb("identf", [128, 128], F32)
    identb = sb("identb", [128, 128], BF16)
    onesb = sb("onesb", [128, 128], BF16)
    gains = sb("gains_s", [128, 64], F32)
    vgT = sb("vgT_s", [128, 16], F32)
    cw = sb("cw_s", [128, 2, 88, 4], F32)
    qg_bc = sb("qg_bc", [128, 128], F32)
    kg_bc = sb("kg_bc", [128, 128], F32)
    cmask = sb("cmask", [128, 128], F32)
    epst = sb("epst", [128, 1], F32)
    thr0 = sb("thr0", [128, 1], F32)
    sqs = [sb("sq%d" % i, [128, 512], BF16) for i in range(4)]
    rbc = sb("rbc", [128, 512], F32)
    hist = sb("hist", [128, 2, 88, 2], F32)
    dummy = sb("dummy_t", [128, 1], F32)
    UW = 14336
    U = sb("U", [128, UW], F32)
    Ub = U[:].bitcast(BF16)

    ps = [ts.enter_context(nc.psum_tensor("ps%d" % i, [128, 512], F32)) for i in range(8)]
    psbf = [p[:].bitcast(BF16) for p in ps]
    psb = [Buf("ps%d" % i) for i in range(8)]

    xTb = [Buf("xT%d" % i) for i in range(16)]
    hTb = [Buf("hT%d" % i) for i in range(16)]
    actb = [[Buf("act%d_%d" % (i, j)) for j in range(4)] for i in range(16)]
    wrb = [Buf("wr%d" % i) for i in range(NSLOT)]
    kTb = [Buf("kT%d" % i) for i in range(17)]
    vtb = [Buf("vt%d" % i) for i in range(17)]
    ikTb = [Buf("ikT%d" % i) for i in range(17)]
    cb = Buf("consts")
    sqb = [Buf("sq%d" % i) for i in range(4)]
    rbcb = Buf("rbc")
    histb = [[Buf("hist%d_%d" % (l, f)) for f in range(88)] for l in range(2)]
    out_bufs = []

    st_ = {"ps": 0, "slot": 0, "sq": 0, "xin": 0, "cnt": 0, "plo": 0, "phi": 8}
    xio = {}

    def ps_next(lo=None, hi=None):
        if lo is None:
            lo, hi = st_["plo"], st_["phi"]
        i = st_["ps"]
        if i < lo or i >= hi:
            i = lo
        st_["ps"] = i + 1 if i + 1 < hi else lo
        return i

    def uid():
        st_["cnt"] += 1
        return st_["cnt"]

    class Scr:
        def __init__(self):
            self.ptr = 0
            self.live = []
            self.fop = None

        def _mk(self, name):
            b = Buf("%s_%d" % (name, uid()))
            b.last_w = self.fop
            self.live.append(b)
            return b

        def f32(self, n, name="s"):
            o = self.ptr
            self.ptr += n
            assert self.ptr <= UW, ("scratch overflow", self.ptr)
            return U[:, o:o + n], self._mk(name)

        def bf(self, n, name="s"):
            w = (n + 1) // 2
            o = self.ptr
            self.ptr += w
            assert self.ptr <= UW, ("scratch overflow", self.ptr)
            return Ub[:, 2 * o:2 * o + n], self._mk(name)

        def extra(self, name="s"):
            return self._mk(name)

        def fence(self, keep=(), mark=0):
            old = [b for b in self.live if b not in keep]
            if old or self.fop is None:
                self.fop = S.add("dve", lambda e: e.memset(dummy[:], 0.0), writes=old)
            self.live = list(keep)
            self.ptr = mark

    scr = Scr()

    def A_act(out, in_, func, reads, writes, **kw):
        return S.add("act", lambda e: e.activation(out=out, in_=in_, func=func, **kw), reads, writes)

    def A_copy(eng, out, in_, reads, writes):
        if eng == "act":
            return S.add("act", lambda e: e.activation(out=out, in_=in_, func=AF.Copy), reads, writes)
        return S.add(eng, lambda e: e.tensor_copy(out=out, in_=in_), reads, writes)

    def A_tt(out, in0, in1, op, reads, writes, eng="dve"):
        return S.add(eng, lambda e: e.tensor_tensor(out=out, in0=in0, in1=in1, op=op), reads, writes)

    def A_ts(out, in0, s1, s2, op0, op1, reads, writes):
        if s2 is None:
            return S.add("dve", lambda e: e.tensor_scalar(out=out, in0=in0, scalar1=s1, scalar2=None, op0=op0), reads, writes)
        return S.add("dve", lambda e: e.tensor_scalar(out=out, in0=in0, scalar1=s1, scalar2=s2, op0=op0, op1=op1), reads, writes)

    def A_stt(out, in0, scalar, in1, op0, op1, reads, writes):
        return S.add("dve", lambda e: e.scalar_tensor_tensor(out=out, in0=in0, scalar=scalar, in1=in1, op0=op0, op1=op1), reads, writes)

    def A_mm(out, lhsT, rhs, start, stop, reads, writes):
        return S.add("pe", lambda e: e.matmul(out, lhsT=lhsT, rhs=rhs, start=start, stop=stop), reads, writes)

    def A_mmg(out, pairs, reads, writes):
        def fn(e):
            n = len(pairs)
            r = None
            for i, (l, rr) in enumerate(pairs):
                r = e.matmul(out, lhsT=l, rhs=rr, start=(i == 0), stop=(i == n - 1))
            return r
        return S.add("pe", fn, reads, writes)

    def A_tr(out, in_, ident, reads, writes):
        return S.add("pe", lambda e: e.transpose(out=out, in_=in_, identity=ident), reads, writes)

    def A_dma(q, out, in_, reads, writes, dbuf):
        return S.add(q, lambda e: e.dma_start(out=out, in_=in_), reads, writes, dma=dbuf)

    def A_store(out, in_, sbuf_buf):
        if sbuf_buf not in out_bufs:
            out_bufs.append(sbuf_buf)
        return S.add("sp", lambda e: e.dma_start(out=out, in_=in_), [sbuf_buf], [], dma=sbuf_buf)

    def wload(src2d, k0, kc, c0, cols):
        i = st_["slot"]
        st_["slot"] = (i + 1) % NSLOT
        view = wring[i][:, 0:kc * cols].rearrange("p (k c) -> p k c", k=kc)
        src = src2d[k0 * 128:(k0 + kc) * 128, c0:c0 + cols].rearrange("(k p) c -> p k c", p=128)
        S.add("pool", lambda e: e.dma_start(out=view, in_=src), [], [wrb[i]], dma=wrb[i])
        return view, wrb[i]

    evq = {"i": 0}

    def ev_eng():
        evq["i"] += 1
        return "act" if evq["i"] % 2 else "dve"

    def mk_ident(e):
        e.memset(identf[:], 0.0)
        e.affine_select(out=identf[:], in_=identf[:], compare_op=ALU.not_equal, fill=1.0, base=0,
                        pattern=[[-1, 128]], channel_multiplier=1)
        e.memset(cmask[:], 0.0)
        return e.affine_select(out=cmask[:], in_=cmask[:], compare_op=ALU.is_ge, fill=NEG, base=0,
                               pattern=[[-1, 128]], channel_multiplier=1)
    S.add("pool", mk_ident, [], [cb])

    def mk_c2(e):
        e.tensor_copy(out=identb[:], in_=identf[:])
        e.memset(onesb[:], 1.0)
        e.memset(epst[:], EPS)
        e.memset(thr0[:], -1.0e29)
        return e.memset(hist[:], 0.0)
    S.add("dve", mk_c2, [cb], [cb] + [b for l in histb for b in l])
    cld = Buf("cload")
    A_dma("sp", gains[:], gains_d[:, :], [], [cb], cld)
    A_dma("sp", vgT[:], vgT_d[:, :], [], [cb], cld)
    A_dma("sp", cw[:].rearrange("p a b c -> p (a b c)"), cw_d[:, :], [], [cb], cld)
    A_dma("sp", qg_bc[:], qg_row.partition_broadcast(128), [], [cb], cld)
    A_dma("sp", kg_bc[:], kg_row.partition_broadcast(128), [], [cb], cld)

    def xin_alloc():
        scr.fence()
        xio["x"] = [scr.f32(D, "xin") for _ in range(2)]

    def load_x(src, row0, NT):
        xin_alloc()
        xins = [a for a, _ in xio["x"]]
        xinb = [b for _, b in xio["x"]]
        for tt in range(NT):
            xi = st_["xin"] % 2
            st_["xin"] += 1
            A_dma("sp", xins[xi], src[row0 + tt * 128:row0 + (tt + 1) * 128, :], [], [xinb[xi]], xinb[xi])
            for q in range(4):
                b = ps_next()
                for j in range(4):
                    dc = 4 * q + j
                    A_tr(ps[b][:, j * 128:(j + 1) * 128], xins[xi][:, dc * 128:(dc + 1) * 128], identf[:],
                         [xinb[xi], cb], [psb[b]])
                A_copy(ev_eng(), xT[:, 4 * q:4 * q + 4, tt * 128:(tt + 1) * 128],
                       ps[b][:].rearrange("p (a b) -> p a b", a=4), [psb[b]], [xTb[4 * q + j] for j in range(4)])

    def store_y(dst, row0, NT):
        xin_alloc()
        xins = [a for a, _ in xio["x"]]
        xinb = [b for _, b in xio["x"]]
        for tt in range(NT):
            xi = st_["xin"] % 2
            st_["xin"] += 1
            for q in range(4):
                b = ps_next()
                for j in range(4):
                    dc = 4 * q + j
                    A_tr(ps[b][:, j * 128:(j + 1) * 128], xT[:, dc, tt * 128:(tt + 1) * 128], identf[:],
                         [xTb[dc], cb], [psb[b]])
                A_copy(ev_eng(), xins[xi][:, 512 * q:512 * (q + 1)], ps[b][:], [psb[b]], [xinb[xi]])
            A_store(dst[row0 + tt * 128:row0 + (tt + 1) * 128, :], xins[xi], xinb[xi])

    def rmsnorm(gi, T):
        bss = ps_next()
        for dc in range(16):
            k = st_["sq"] % 4
            st_["sq"] += 1
            A_act(sqs[k][:, :T], xT[:, dc, :T], AF.Square, [xTb[dc]], [sqb[k]])
            A_mm(ps[bss][:, :T], onesb[:], sqs[k][:, :T], dc == 0, dc == 15, [sqb[k], cb], [psb[bss]])
        A_act(rbc[:, :T], ps[bss][:, :T], AF.Sqrt, [psb[bss], cb], [rbcb], scale=1.0 / D, bias=epst[:])
        S.add("dve", lambda e: e.reciprocal(out=rbc[:, :T], in_=rbc[:, :T]), [rbcb], [rbcb])
        for dc in range(16):
            A_stt(hT[:, dc, :T], xT[:, dc, :T], gains[:, gi * 16 + dc:gi * 16 + dc + 1], rbc[:, :T],
                  ALU.mult, ALU.mult, [xTb[dc], rbcb, cb], [hTb[dc]])

    def proj_b_residual(w2d, K, T, rhs_of, rhs_bufs_of, k0=0):
        for j in range(8):
            W, wb = wload(w2d, k0, K, 256 * j, 256)
            for fc in range(2):
                dco = 2 * j + fc
                b = ps_next()
                A_mmg(ps[b][:, :T], [(W[:, k, fc * 128:(fc + 1) * 128], rhs_of(k)) for k in range(K)],
                      [wb] + [x for k in range(K) for x in rhs_bufs_of(k)], [psb[b]])
                A_tt(xT[:, dco, :T], ps[b][:, :T], xT[:, dco, :T], ALU.add, [psb[b], xTb[dco]], [xTb[dco]])

    def act_bufs(ch, NT):
        return [actb[ch][t] for t in range(NT)]

    def mixer_a(T, NT, sample):
        scr.fence()
        rmsnorm(0, T)
        wsf, wsfb = scr.f32(2048, "wsf")
        wsb_, wsbb = scr.bf(2048, "wsb")
        bsb_, bsbb = scr.f32(2048, "bsb")
        A_dma("sp", wsf, (wsT_s if sample else wsT_p)[:, :], [], [wsfb], wsfb)
        A_dma("sp", bsb_, (bs_s if sample else bs_p).partition_broadcast(128), [], [bsbb], bsbb)
        wsf3 = wsf.rearrange("p (g t) -> p g t", g=16)
        S.add("pool", lambda e: e.affine_select(out=wsf3, in_=wsf3, compare_op=ALU.is_ge, fill=0.0, base=0,
                                                pattern=[[0, 16], [1, 128]], channel_multiplier=-1),
              [wsfb], [wsfb])
        A_copy("dve", wsb_, wsf, [wsfb], [wsbb])
        ws3 = wsb_.rearrange("p (g t) -> p g t", g=16)
        bs3 = bsb_.rearrange("p (g t) -> p g t", g=16)
        if sample:
            vraw, vrawb = scr.f32(2048, "vraw")
            vn, vnb = scr.bf(2048, "vn")
            vgbc, vgbcb = scr.f32(2048, "vgbc")
            vout, voutb = scr.f32(2048, "vout")
            A_dma("sp", vgbc, vg_row.partition_broadcast(128), [], [vgbcb], vgbcb)
            vraws = [vraw]
            vns = [vn]
            vrb = [vrawb]
            vnbs = [vnb]
            junk, junkb = vout, voutb
        else:
            vns, vnbs = [], []
            for tt in range(NT):
                a, b_ = scr.bf(2048, "vraw")
                vns.append(a)
                vnbs.append(b_)
            vraws, vrb = vns, vnbs
            junk, junkb = scr.bf(2048, "junk")
        ssq, ssqb = scr.f32(4, "ssq")
        tmps = [scr.f32(T, "tmpA") for _ in range(2)]
        for j in range(8):
            W, wb = wload(a_w_in, 0, 16, 256 * j, 256)
            for fc in range(2):
                g = 2 * j + fc
                b = ps_next()
                A_mmg(ps[b][:, :T], [(W[:, k, fc * 128:(fc + 1) * 128], hT[:, k, :T]) for k in range(16)],
                      [wb] + hTb, [psb[b]])
                A_act(actT[:, g, :T], ps[b][:, :T], AF.Gelu_apprx_tanh, [psb[b]], act_bufs(g, NT))
        for c in range(4):
            WA, wa = wload(a_w_in, 0, 8, 2048 + 512 * c, 512)
            WB, wb2 = wload(a_w_in, 8, 8, 2048 + 512 * c, 512)
            for tt in range(NT):
                b = ps_next()
                A_mmg(ps[b][:, :], [(hT[:, k, tt * 128:(tt + 1) * 128], (WA if k < 8 else WB)[:, k % 8, :]) for k in range(16)],
                      [wa, wb2] + hTb, [psb[b]])
                A_act(vraws[tt][:, 512 * c:512 * (c + 1)], ps[b][:, :], AF.Gelu_apprx_tanh, [psb[b]], [vrb[tt]])
        for tt in range(NT):
            A_act(junk, vraws[tt], AF.Square, [vrb[tt]], [junkb, ssqb], accum_out=ssq[:, tt:tt + 1])
        A_act(ssq[:, :NT], ssq[:, :NT], AF.Sqrt, [ssqb, cb], [ssqb], scale=1.0 / D, bias=epst[:])
        S.add("dve", lambda e: e.reciprocal(out=ssq[:, :NT], in_=ssq[:, :NT]), [ssqb], [ssqb])
        for tt in range(NT):
            if sample:
                A_stt(vout, vraw, ssq[:, 0:1], vgbc, ALU.mult, ALU.mult, [vrawb, ssqb, vgbcb], [voutb])
                A_store(ncv_s[:, :], vout, voutb)
                A_ts(vn, vraw, ssq[:, 0:1], None, ALU.mult, None, [vrawb, ssqb], [vnb])
            else:
                A_ts(vns[tt], vns[tt], ssq[:, tt:tt + 1], None, ALU.mult, None, [vnbs[tt], ssqb], [vnbs[tt]])
        for g in range(16):
            b = ps_next()
            for tt in range(NT):
                A_mm(ps[b][:, tt * 128:(tt + 1) * 128], vns[tt][:, g * 128:(g + 1) * 128], ws3[:, g, :], True, True,
                     [vnbs[tt], wsbb], [psb[b]])
            tm, tmb = tmps[g % 2]
            A_stt(tm[:, :T].rearrange("p (a t) -> p a t", a=NT), ps[b][:, :T].rearrange("p (a t) -> p a t", a=NT),
                  vgT[:, g:g + 1], bs3[:, g:g + 1, :].to_broadcast([128, NT, 128]), ALU.mult, ALU.add,
                  [psb[b], bsbb, cb], [tmb])
            A_tt(actT[:, g, :T], tm[:, :T], actT[:, g, :T], ALU.mult, [tmb] + act_bufs(g, NT), act_bufs(g, NT))
        proj_b_residual(a_w_out, 16, T, lambda k: actT[:, k, :T], lambda k: act_bufs(k, NT))

    def conv_ffn(layer, gi, T, NT, sample, last_prompt):
        scr.fence()
        rmsnorm(gi, T)
        nb, tl = (16, 8) if sample else (1, T)
        AW = nb * (tl + 2)
        aext = [[scr.f32(AW, "aext") for _ in range(2)] for _ in range(2)]
        cbuf = [[scr.f32(512, "c") for _ in range(2)] for _ in range(2)]
        sgs = [scr.f32(512, "sg") for _ in range(2)]
        hss = [scr.f32(32, "hs") for _ in range(8)]
        ssts = [scr.f32(512, "sst") for _ in range(2)]
        osts = [scr.f32(512, "ost") for _ in range(2)]
        w2 = f_w_in[layer]
        wo = f_w_out[layer]
        ncv_dst = nconv_s if sample else nconv_p
        want_out = sample or last_prompt
        hidx = 0
        par = 0
        groups = [(0, 16), (16, 16), (32, 12)]
        for (F0, nF) in groups:
            for U_ in range(F0 // 2, (F0 + nF) // 2):
                Wg, wgb = wload(w2, 0, 16, 256 * U_, 256)
                Wu, wub = wload(w2, 0, 16, DFF + 256 * U_, 256)
                if sample:
                    sst, sstb = ssts[U_ % 2]
                    sstv = sst[0:32, :]
                    S.add("sp", lambda e, sstv=sstv, U_=U_: [
                        e.dma_start(out=sstv[:, 0:256], in_=conv_s[layer, :, 256 * U_:256 * (U_ + 1)]),
                        e.dma_start(out=sstv[:, 256:512], in_=conv_s[layer, :, DFF + 256 * U_:DFF + 256 * (U_ + 1)])],
                        [], [sstb], dma=sstb, ndma=2)
                    bh = ps_next()
                    for j in range(4):
                        A_tr(ps[bh][:, j * 32:(j + 1) * 32], sstv[:, j * 128:(j + 1) * 128], identf[0:32, 0:32],
                             [sstb, cb], [psb[bh]])
                hs_used = []
                for fc in range(2):
                    F = 2 * U_ + fc
                    cs = []
                    for half, (W, wbuf) in enumerate(((Wg, wgb), (Wu, wub))):
                        Fp = F + 44 * half
                        ae, aeb = aext[half][par]
                        ae3 = ae.rearrange("p (b t) -> p b t", b=nb)
                        cc, ccb = cbuf[half][par]
                        b = ps_next()
                        A_mmg(ps[b][:, :T], [(W[:, k, fc * 128:(fc + 1) * 128], hT[:, k, :T]) for k in range(16)],
                              [wbuf] + hTb, [psb[b]])
                        if sample:
                            jj = half * 2 + fc
                            A_copy("act", ae3[:, :, 0:2], ps[bh][:, jj * 32:(jj + 1) * 32].rearrange("p (b j) -> p b j", b=16),
                                   [psb[bh]], [aeb])
                        else:
                            A_copy("act", ae3[:, :, 0:2], hist[:, layer, Fp:Fp + 1, :], [histb[layer][Fp]], [aeb])
                        A_copy("act", ae3[:, :, 2:tl + 2], ps[b][:, :T].rearrange("p (b t) -> p b t", b=nb), [psb[b]], [aeb])
                        c3 = cc[:, :T].rearrange("p (b t) -> p b t", b=nb)
                        A_act(cc[:, :T], ps[b][:, :T], AF.Identity, [psb[b], cb], [ccb],
                              scale=cw[:, layer, Fp, 2:3], bias=cw[:, layer, Fp, 3:4])
                        A_stt(c3, ae3[:, :, 0:tl], cw[:, layer, Fp, 0:1], c3, ALU.mult, ALU.add, [aeb, ccb, cb], [ccb])
                        A_stt(c3, ae3[:, :, 1:tl + 1], cw[:, layer, Fp, 1:2], c3, ALU.mult, ALU.add, [aeb, ccb, cb], [ccb])
                        if sample:
                            hs, hsb = hss[hidx % 8]
                            hidx += 1
                            A_copy("dve", hs[:, 0:32].rearrange("p (b j) -> p b j", b=16), ae3[:, :, tl:tl + 2], [aeb], [hsb])
                            hs_used.append((hs[:, 0:32], hsb))
                        else:
                            A_copy("dve", hist[:, layer, Fp:Fp + 1, :], ae3[:, :, tl:tl + 2], [aeb], [histb[layer][Fp]])
                            hs_used.append((hist[:, layer, Fp, :], histb[layer][Fp]))
                        cs.append((cc, ccb))
                    sg, sgb = sgs[par]
                    A_act(sg[:, :T], cs[0][0][:, :T], AF.Silu, [cs[0][1]], [sgb])
                    A_tt(actT[:, F - F0, :T], sg[:, :T], cs[1][0][:, :T], ALU.mult, [sgb, cs[1][1]], act_bufs(F - F0, NT))
                    par ^= 1
                if want_out:
                    order = [hs_used[0], hs_used[2], hs_used[1], hs_used[3]]
                    bo = ps_next()
                    nr = 2 * nb
                    for j, (hv, hb) in enumerate(order):
                        A_tr(ps[bo][0:nr, j * 128:(j + 1) * 128], hv, identf[:], [hb, cb], [psb[bo]])
                    ost, ostb = osts[U_ % 2]
                    A_copy("act", ost[0:nr, :], ps[bo][0:nr, :], [psb[bo]], [ostb])
                    if ostb not in out_bufs:
                        out_bufs.append(ostb)
                    S.add("sp", lambda e, ost=ost, U_=U_, nr=nr: [
                        e.dma_start(out=ncv_dst[layer, :, 256 * U_:256 * (U_ + 1)], in_=ost[0:nr, 0:256]),
                        e.dma_start(out=ncv_dst[layer, :, DFF + 256 * U_:DFF + 256 * (U_ + 1)], in_=ost[0:nr, 256:512])],
                        [ostb], [], dma=ostb, ndma=2)
            proj_b_residual(wo, nF, T, lambda k: actT[:, k, :T], lambda k: act_bufs(k, NT), k0=F0)

    def rope(dst, src, cos, sin, H, hd, t1, t2, reads, writes, tb):
        s4 = src.rearrange("p (h two d) -> p h two d", h=H, two=2)
        d4 = dst.rearrange("p (h two d) -> p h two d", h=H, two=2)
        x1, x2 = s4[:, :, 0, :], s4[:, :, 1, :]
        cb_ = cos[:, None, :].to_broadcast([128, H, hd])
        sb_ = sin[:, None, :].to_broadcast([128, H, hd])
        a = t1.rearrange("p (h d) -> p h d", h=H)
        b_ = t2.rearrange("p (h d) -> p h d", h=H)

        def fn(e):
            e.tensor_tensor(out=a, in0=x1, in1=cb_, op=ALU.mult)
            e.tensor_tensor(out=b_, in0=x2, in1=sb_, op=ALU.mult)
            return e.tensor_tensor(out=d4[:, :, 0, :], in0=a, in1=b_, op=ALU.subtract)
        S.add("dve", fn, reads, [tb] + writes)

        def fn2(e):
            e.tensor_tensor(out=a, in0=x2, in1=cb_, op=ALU.mult)
            e.tensor_tensor(out=b_, in0=x1, in1=sb_, op=ALU.mult)
            return e.tensor_tensor(out=d4[:, :, 1, :], in0=a, in1=b_, op=ALU.add)
        S.add("dve", fn2, reads + [tb], [tb] + writes)

    def topk_thr(SC, SCb, WK, WKb, ncols, m8, m8b):
        for r in range(32):
            src, srcb = (SC, SCb) if r == 0 else (WK, WKb)
            S.add("dve", lambda e, src=src: e.max(out=m8, in_=src[:, :ncols]), [srcb], [m8b])
            if r < 31:
                S.add("dve", lambda e, src=src: e.match_replace(out=WK[:, :ncols], in_to_replace=m8, in_values=src[:, :ncols],
                                                                 imm_value=-3.0e38), [srcb, m8b], [WKb])
            yield

    def mixer_b(g, T, NT, sample):
        pos0 = SEQ if sample else 512 * g
        scr.fence()
        rmsnorm(2, T)
        R = {}
        if sample:
            R["iqTs"] = [scr.bf(16 * 128, "iqTs") for _ in range(2)]
            kTn_, _ = scr.bf(512, "kTn")
            R["kTn3"] = kTn_.rearrange("p (a s) -> p a s", a=4)
            R["vnew"], _ = scr.bf(512, "vnew")
            R["ikTn"], _ = scr.bf(128, "ikTn")
            R["oS"], R["oSb"] = scr.bf(16 * 128, "oS")
            eepos, eeb = scr.bf(248, "eepos")
            eeneg, _ = scr.bf(248, "eeneg")
            nmask, nmb = scr.f32(128, "nmask")
            idxf, idxb = scr.f32(256, "idx")
            idxall = idxf.bitcast(I32)
            R.update(eepos=eepos, eeneg=eeneg, eeb=eeb, nmask=nmask, nmb=nmb, idxall=idxall, idxb=idxb)
        else:
            iqT, _ = scr.bf(8 * 512, "iqT")
            R["iqT3"] = iqT.rearrange("p (a t) -> p a t", a=8)
            R["iqTbs"] = [scr.extra("iqTt") for _ in range(NT)]
        iwt, iwtb = scr.f32(NT * 16, "iwt")
        R["iwt"], R["iwtb"] = iwt, iwtb
        R["keep"] = list(scr.live)
        R["mark"] = scr.ptr
        if sample:
            iqx, iqxb = scr.f32(1024, "iqx")
            eef, eefb = scr.f32(248, "eef")
            ptbf, ptbb = scr.f32(256, "ptb")
            pcol, pcolb = scr.f32(1, "pcol")
            A_dma("sp", eef, c_ee[:, :], [], [eefb], eefb)
            A_dma("sp", nmask, c_newmask[:, :], [], [nmb], nmb)
            A_dma("sp", ptbf.bitcast(I32), ptab.partition_broadcast(128), [], [ptbb], ptbb)
            S.add("pool", lambda e: e.iota(pcol, pattern=[[0, 1]], base=0, channel_multiplier=1,
                                           allow_small_or_imprecise_dtypes=True), [], [pcolb])

            def mk_ee(e):
                e.tensor_copy(out=eepos, in_=eef)
                e.tensor_scalar(out=eeneg, in0=eef, scalar1=-1.0, scalar2=None, op0=ALU.mult)
                return e.tensor_scalar(out=idxall, in0=ptbf.bitcast(I32), scalar1=128.0, scalar2=pcol[:, 0:1],
                                       op0=ALU.mult, op1=ALU.add)
            S.add("dve", mk_ee, [eefb, ptbb, pcolb], [eeb, idxb])
        rq, rqb = scr.f32(NT * 128, "rq")
        ri, rib = scr.f32(NT * 64, "ri")
        A_dma("sp", rq.rearrange("p (a c) -> p a c", a=NT), ropeqk[pos0:pos0 + T, :].rearrange("(a p) c -> p a c", p=128), [], [rqb], rqb)
        A_dma("sp", ri.rearrange("p (a c) -> p a c", a=NT), ropei[pos0:pos0 + T, :].rearrange("(a p) c -> p a c", p=128), [], [rib], rib)
        sqf = [scr.f32(512, "sqf") for _ in range(2)]
        qn = [scr.f32(512, "qn") for _ in range(2)]
        t12 = [(scr.f32(256, "t1"), scr.f32(256, "t2")) for _ in range(2)]
        qr = [scr.bf(512, "qr") for _ in range(2)]
        kr = [scr.f32(512, "kr") for _ in range(2)]
        vf = [scr.f32(512, "vf") for _ in range(2)]
        ikr = [scr.f32(64, "ikr") for _ in range(2)]
        ikd = [scr.bf(128, "ikd") for _ in range(2)]
        ss4 = [scr.f32(8, "ss4") for _ in range(2)]
        cnt = 0
        for cc in range(9):
            ncol = 512 if cc < 8 else 80
            WA, wa = wload(b_w_in, 0, 8, 512 * cc, ncol)
            WB, wb2 = wload(b_w_in, 8, 8, 512 * cc, ncol)
            for tt in range(NT):
                cnt += 1
                p2 = cnt % 2
                blk = 16 if sample else 4 * g + tt
                tcol = pos0 + tt * 128
                b = ps_next()
                A_mmg(ps[b][:, :ncol], [(hT[:, k, tt * 128:(tt + 1) * 128], (WA if k < 8 else WB)[:, k % 8, :]) for k in range(16)],
                      [wa, wb2] + hTb, [psb[b]])
                cosq = rq[:, tt * 128:tt * 128 + 64]
                sinq = rq[:, tt * 128 + 64:tt * 128 + 128]
                cosi = ri[:, tt * 64:tt * 64 + 32]
                sini = ri[:, tt * 64 + 32:tt * 64 + 64]
                (t1, t1b), (t2, _) = t12[p2]
                if cc <= 4:
                    sq_, sq_b = sqf[p2]
                    s4, s4b = ss4[p2]
                    qn_, qnb = qn[p2]
                    A_act(sq_, ps[b][:, :], AF.Square, [psb[b]], [sq_b])
                    S.add("dve", lambda e, s4=s4, sq_=sq_: e.reduce_sum(out=s4[:, 0:4], in_=sq_.rearrange("p (h d) -> p h d", h=4), axis=AX.X),
                          [sq_b], [s4b])
                    A_act(s4[:, 0:4], s4[:, 0:4], AF.Sqrt, [s4b, cb], [s4b], scale=1.0 / 128, bias=epst[:])
                    S.add("dve", lambda e, s4=s4: e.reciprocal(out=s4[:, 0:4], in_=s4[:, 0:4]), [s4b], [s4b])
                    qn3 = qn_.rearrange("p (h d) -> p h d", h=4)
                    A_tt(qn3, ps[b][:, :].rearrange("p (h d) -> p h d", h=4), s4[:, 0:4, None].to_broadcast([128, 4, 128]),
                         ALU.mult, [psb[b], s4b], [qnb])
                    gbc = qg_bc if cc < 4 else kg_bc
                    A_tt(qn3, qn3, gbc[:, None, :].to_broadcast([128, 4, 128]), ALU.mult, [qnb, cb], [qnb])
                    qr_, qrb = qr[p2]
                    if cc < 4:
                        rope(qr_, qn_, cosq, sinq, 4, 64, t1, t2, [qnb, rqb], [qrb], t1b)
                    else:
                        kr_, krb = kr[p2]
                        rope(kr_, qn_, cosq, sinq, 4, 64, t1, t2, [qnb, rqb], [krb], t1b)
                        A_store(nk_s[:, :] if sample else nk_p[tcol:tcol + 128, :], kr_, krb)
                        A_copy("act", qr_, kr_, [krb], [qrb])
                    bt = ps_next()
                    for j in range(4):
                        A_tr(psbf[bt][:, j * 128:(j + 1) * 128], qr_[:, j * 128:(j + 1) * 128], identb[:], [qrb, cb], [psb[bt]])
                    src3 = psbf[bt][:, 0:512].rearrange("p (a t) -> p a t", a=4)
                    if cc < 4:
                        A_copy(ev_eng(), actT[:, 4 * cc:4 * cc + 4, tt * 128:(tt + 1) * 128], src3, [psb[bt]],
                               [actb[4 * cc + j][tt] for j in range(4)])
                    elif sample:
                        A_copy(ev_eng(), R["kTn3"], src3, [psb[bt]], [kTb[16]])
                    else:
                        A_copy(ev_eng(), kT[:, :, tcol:tcol + 128], src3, [psb[bt]], [kTb[blk]])
                elif cc == 5:
                    vf_, vfb = vf[p2]
                    A_copy("act", vf_, ps[b][:, :], [psb[b]], [vfb])
                    A_store(nv_s[:, :] if sample else nv_p[tcol:tcol + 128, :], vf_, vfb)
                    if sample:
                        A_copy("dve", R["vnew"], vf_, [vfb], [vtb[16]])
                    else:
                        A_copy("dve", vtok[:, blk, :], vf_, [vfb], [vtb[blk]])
                elif cc in (6, 7):
                    if sample:
                        rope(iqx[:, 512 * (cc - 6):512 * (cc - 5)], ps[b][:, :], cosi, sini, 8, 32, t1, t2, [psb[b], rib], [iqxb], t1b)
                    else:
                        qr_, qrb = qr[p2]
                        rope(qr_, ps[b][:, :], cosi, sini, 8, 32, t1, t2, [psb[b], rib], [qrb], t1b)
                        bt = ps_next()
                        for j in range(4):
                            A_tr(psbf[bt][:, j * 128:(j + 1) * 128], qr_[:, j * 128:(j + 1) * 128], identb[:], [qrb, cb], [psb[bt]])
                        A_copy(ev_eng(), R["iqT3"][:, 4 * (cc - 6):4 * (cc - 6) + 4, tt * 128:(tt + 1) * 128],
                               psbf[bt][:, 0:512].rearrange("p (a t) -> p a t", a=4), [psb[bt]], [R["iqTbs"][tt]])
                else:
                    ik_, ikb_ = ikr[p2]
                    rope(ik_, ps[b][:, 0:64], cosi, sini, 1, 32, t1[:, 0:32], t2[:, 0:32], [psb[b], rib], [ikb_], t1b)
                    A_store(nik_s[:, :] if sample else nik_p[tcol:tcol + 128, :], ik_, ikb_)
                    A_act(iwt[:, tt * 16:(tt + 1) * 16], ps[b][:, 64:80], AF.Copy, [psb[b]], [iwtb], scale=IDX_W_SCALE)
                    ikd_, ikdb = ikd[p2]
                    S.add("dve", lambda e, ikd_=ikd_, ik_=ik_: (e.tensor_copy(out=ikd_[:, 0:64], in_=ik_), e.tensor_copy(out=ikd_[:, 64:128], in_=ik_))[1],
                          [ikb_], [ikdb])
                    bt = ps_next()
                    A_tr(psbf[bt][:, 0:128], ikd_, identb[:], [ikdb, cb], [psb[bt]])
                    if sample:
                        A_copy(ev_eng(), R["ikTn"], psbf[bt][:, 0:128], [psb[bt]], [ikTb[16]])
                    else:
                        A_copy(ev_eng(), ikT[:, tcol:tcol + 128], psbf[bt][:, 0:128], [psb[bt]], [ikTb[blk]])
        if sample:
            iwp, iwpb = scr.f32(16, "iwp")
            iwm, iwmb = scr.f32(16, "iwm")
            A_ts(iwp, iwt[:, 0:16], 0.0, None, ALU.max, None, [iwtb], [iwpb])
            A_ts(iwm, iwt[:, 0:16], -1.0, 0.0, ALU.mult, ALU.max, [iwtb], [iwmb])
            iqpm, iqpmb = scr.bf(1024, "iqpm")
            for sg_, (iw_, iwb_) in enumerate(((iwp, iwpb), (iwm, iwmb))):
                A_tt(iqpm.rearrange("p (h d) -> p h d", h=16), iqx.rearrange("p (h d) -> p h d", h=16),
                     iw_[:, 0:16, None].to_broadcast([128, 16, 64]), ALU.mult, [iqxb, iwb_], [iqpmb])
                dst_, dstb_ = R["iqTs"][sg_]
                d4 = dst_.rearrange("p (b h t) -> p b h t", b=16, h=16)
                for hf in range(2):
                    bt = ps_next()
                    for j in range(8):
                        h = 8 * hf + j
                        A_tr(psbf[bt][0:64, j * 128:(j + 1) * 128], iqpm[:, h * 64:(h + 1) * 64], identb[:], [iqpmb, cb], [psb[bt]])
                    A_copy(ev_eng(), d4[0:64, :, 8 * hf:8 * hf + 8, :],
                           psbf[bt][0:64, 0:1024].rearrange("p (h b t) -> p b h t", h=8, b=16), [psb[bt]], [dstb_])
        return R

    def prompt_attention(g, T, NT, R):
        iqT3, iqTbs, iwt, iwtb = R["iqT3"], R["iqTbs"], R["iwt"], R["iwtb"]
        scr.fence(keep=R["keep"], mark=R["mark"])
        SC, SCb = scr.f32(2048, "SC")
        WK, WKb = scr.f32(2048, "WK")
        sl, slb = scr.bf(2048, "sel")
        selT = [scr.bf(2048, "selT") for _ in range(2)]
        m8, m8b = scr.f32(8, "m8")
        rl = [scr.f32(512, "rl") for _ in range(3)]
        et = [scr.bf(512, "et") for _ in range(3)]
        pt = [scr.bf(512, "pt") for _ in range(3)]
        rs, rsb = scr.f32(512, "rs")
        kq = {"k": 0}

        def scores(tt):
            ig = 4 * g + tt
            ncols = 128 * (ig + 1)
            for c in range((ncols + 511) // 512):
                w = min(512, ncols - 512 * c)
                for h in range(16):
                    hh, pr = h % 2, h // 2
                    b = ps_next()
                    A_mm(ps[b][:, :w], iqT3[64 * hh:64 * hh + 64, pr, tt * 128:(tt + 1) * 128], ikT[64 * hh:64 * hh + 64, 512 * c:512 * c + w],
                         True, True, [iqTbs[tt]] + ikTb[4 * c:4 * c + (w // 128)], [psb[b]])
                    r_, rb_ = rl[kq["k"] % 3]
                    kq["k"] += 1
                    A_act(r_[:, :w], ps[b][:, :w], AF.Relu, [psb[b]], [rb_])
                    if h == 0:
                        A_ts(SC[:, 512 * c:512 * c + w], r_[:, :w], iwt[:, tt * 16:tt * 16 + 1], None, ALU.mult, None, [rb_, iwtb], [SCb])
                    else:
                        A_stt(SC[:, 512 * c:512 * c + w], r_[:, :w], iwt[:, tt * 16 + h:tt * 16 + h + 1], SC[:, 512 * c:512 * c + w],
                              ALU.mult, ALU.add, [rb_, iwtb, SCb], [SCb])
                    yield
            if SUB < 2:
                return
            A_tt(SC[:, ncols - 128:ncols], SC[:, ncols - 128:ncols], cmask[:], ALU.add, [SCb, cb], [SCb])
            if ig >= 2:
                for _ in topk_thr(SC, SCb, WK, WKb, ncols, m8, m8b):
                    yield
                thr = m8[:, 7:8]
            else:
                thr = thr0[:, 0:1]
            A_ts(sl[:, :ncols], SC[:, :ncols], thr, None, ALU.is_ge, None, [SCb, m8b, cb], [slb])
            sT, sTb = selT[tt % 2]
            for q in range((ig + 4) // 4):
                nb_ = min(4, ig + 1 - 4 * q)
                bt = ps_next()
                for j in range(nb_):
                    A_tr(psbf[bt][:, j * 128:(j + 1) * 128], sl[:, (4 * q + j) * 128:(4 * q + j + 1) * 128], identb[:], [slb, cb], [psb[bt]])
                A_copy(ev_eng(), sT[:, 512 * q:512 * q + 128 * nb_], psbf[bt][:, 0:128 * nb_], [psb[bt]], [sTb])
                yield

        def attend(tt):
            ig = 4 * g + tt
            nblk = ig + 1
            sT, sTb = selT[tt % 2]
            sT3 = sT.rearrange("p (a q) -> p a q", a=16)
            kk = 0
            for kv in range(4):
                qb = [actb[4 * kv + j][tt] for j in range(4)]
                bo = ps_next(0, 4)
                bs_ = ps_next(0, 4)
                pend = None

                def qk(blk, kk):
                    b = ps_next()
                    A_mm(ps[b][:, :], kT[:, kv, blk * 128:(blk + 1) * 128], actT[:, 4 * kv:4 * kv + 4, tt * 128:(tt + 1) * 128],
                         True, True, [kTb[blk]] + qb, [psb[b]])
                    e_, eb_ = et[kk % 3]
                    A_act(e_, ps[b][:, :], AF.Exp, [psb[b]], [eb_], scale=ATT_SCALE)
                    p_, pb_ = pt[kk % 3]
                    A_tt(p_.rearrange("p (a q) -> p a q", a=4), e_.rearrange("p (a q) -> p a q", a=4),
                         sT3[:, blk:blk + 1, :].to_broadcast([128, 4, 128]), ALU.mult, [eb_, sTb], [pb_])
                    return (blk, p_, pb_)

                def pv(item):
                    blk, p_, pb_ = item
                    if SUB < 4:
                        return
                    A_mm(ps[bo][:, :], vtok[:, blk, kv * 128:(kv + 1) * 128], p_, blk == 0, blk == nblk - 1, [vtb[blk], pb_], [psb[bo]])
                    A_mm(ps[bs_][:, :], onesb[:], p_, blk == 0, blk == nblk - 1, [pb_, cb], [psb[bs_]])

                for blk in range(nblk):
                    item = qk(blk, kk)
                    kk += 1
                    if pend is not None:
                        pv(pend)
                    pend = item
                    yield
                pv(pend)
                if SUB < 5:
                    yield
                    continue
                A_copy("act", rs, ps[bs_][:, :], [psb[bs_]], [rsb])
                S.add("dve", lambda e: e.reciprocal(out=rs, in_=rs), [rsb], [rsb])
                A_tt(actT[:, 4 * kv:4 * kv + 4, tt * 128:(tt + 1) * 128], ps[bo][:, :].rearrange("p (a q) -> p a q", a=4),
                     rs.rearrange("p (a q) -> p a q", a=4), ALU.mult, [psb[bo], rsb], qb)
                yield

        if not PIPE:
            for tt in range(NT):
                for _ in scores(tt):
                    pass
                for _ in attend(tt):
                    pass
            return
        for _ in scores(0):
            pass
        st_["plo"], st_["phi"] = 4, 8
        st_["ps"] = 4
        for tt in range(NT):
            ga = attend(tt) if SUB >= 3 else iter(())
            gs = scores(tt + 1) if tt + 1 < NT else iter(())
            da = ds = False
            while not (da and ds):
                if not da:
                    try:
                        next(ga)
                    except StopIteration:
                        da = True
                for _ in range(3):
                    if not ds:
                        try:
                            next(gs)
                        except StopIteration:
                            ds = True
        st_["plo"], st_["phi"] = 0, 8

    def sample_attention(R):
        iqTs, kTn3, vnew, ikTn = R["iqTs"], R["kTn3"], R["vnew"], R["ikTn"]
        eepos, eeneg, eeb, nmask, nmb, idxall, idxb = R["eepos"], R["eeneg"], R["eeb"], R["nmask"], R["nmb"], R["idxall"], R["idxb"]
        keep, mark = R["keep"], R["mark"]
        scr.fence(keep=keep, mark=mark)
        ikg = [scr.bf(16 * 64, "ikg") for _ in range(2)]
        ikTs = [scr.bf(2048, "ikTs") for _ in range(2)]
        rr = [scr.bf(512, "rr") for _ in range(3)]
        k = 0
        for b in range(NSEQ):
            g_, gb_ = ikg[b % 2]
            g3 = g_.rearrange("p (j d) -> p j d", j=16)
            S.add("pool", lambda e, b=b, g3=g3: [
                e.indirect_dma_start(out=g3[:, j, :], out_offset=None, in_=cik[:, :],
                                     in_offset=bass.IndirectOffsetOnAxis(ap=idxall[:, b * 16 + j:b * 16 + j + 1], axis=0))
                for j in range(16)], [idxb], [gb_], dma=gb_, ndma=16)
            it_, itb_ = ikTs[b % 2]
            for hf in range(2):
                bt = ps_next(5, 8)
                for j in range(8):
                    A_tr(psbf[bt][0:64, j * 128:(j + 1) * 128], g3[:, 8 * hf + j, :], identb[:], [gb_, cb], [psb[bt]])
                A_copy(ev_eng(), it_[0:64, 1024 * hf:1024 * (hf + 1)], psbf[bt][0:64, 0:1024], [psb[bt]], [itb_])
            for c in range(5):
                for sg_ in range(2):
                    iq_, iqb_ = iqTs[sg_]
                    iq3 = iq_.rearrange("p (b m) -> p b m", b=16)
                    bl = ps_next(5, 8)
                    if c < 4:
                        rhs, rb = it_[0:64, 512 * c:512 * (c + 1)], [itb_]
                        w = 512
                    else:
                        rhs, rb = ikTn[0:64, :], [ikTb[16]]
                        w = 128
                    A_mm(ps[bl][:, :w], iq3[0:64, b, :], rhs, True, True, [iqb_] + rb, [psb[bl]])
                    r_, rb_ = rr[k % 3]
                    k += 1
                    A_act(r_[:, :w], ps[bl][:, :w], AF.Relu, [psb[bl]], [rb_])
                    ee = (eepos if sg_ == 0 else eeneg)[:, 120 - 8 * b:248 - 8 * b]
                    A_mm(ps[c][:, :w], ee, r_[:, :w], (b == 0 and sg_ == 0), (b == NSEQ - 1 and sg_ == 1), [eeb, rb_], [psb[c]])
        if SUB < 2:
            return
        scr.fence(keep=keep, mark=mark)
        NC_ = 2048 + 128
        SC, SCb = scr.f32(NC_, "SCs")
        WK, WKb = scr.f32(NC_, "WKs")
        sl, slb = scr.bf(NC_, "sels")
        sT, sTb = scr.bf(17 * 128, "selTs")
        m8, m8b = scr.f32(8, "m8s")
        for c in range(5):
            w = 512 if c < 4 else 128
            A_copy(ev_eng(), SC[:, 512 * c:512 * c + w], ps[c][:, :w], [psb[c]], [SCb])
        A_tt(SC[:, 2048:NC_], SC[:, 2048:NC_], nmask, ALU.add, [SCb, nmb], [SCb])
        for _ in topk_thr(SC, SCb, WK, WKb, NC_, m8, m8b):
            pass
        A_ts(sl, SC, m8[:, 7:8], None, ALU.is_ge, None, [SCb, m8b], [slb])
        for q in range(5):
            nb_ = 4 if q < 4 else 1
            bt = ps_next()
            for j in range(nb_):
                A_tr(psbf[bt][:, j * 128:(j + 1) * 128], sl[:, (4 * q + j) * 128:(4 * q + j + 1) * 128], identb[:], [slb, cb], [psb[bt]])
            A_copy(ev_eng(), sT[:, 512 * q:512 * q + 128 * nb_], psbf[bt][:, 0:128 * nb_], [psb[bt]], [sTb])
        sT3 = sT.rearrange("p (a q) -> p a q", a=17)
        if SUB < 3:
            return
        kgs = [scr.bf(512, "kg") for _ in range(3)]
        vgs = [scr.bf(512, "vg") for _ in range(3)]
        kTbs = [scr.bf(512, "kTb") for _ in range(2)]
        ets = [scr.bf(128, "ets") for _ in range(3)]
        pts = [scr.bf(128, "pts") for _ in range(3)]
        rs, rsb = scr.f32(128, "rss")
        it = 0
        for b in range(NSEQ):
            qb = [actb[h][0] for h in range(16)]
            bo = ps_next(0, 2)
            bs_ = ps_next(0, 2)
            for blk in range(17):
                it += 1
                if blk < 16:
                    kg_, kgb_ = kgs[it % 3]
                    vg_, vgb_ = vgs[it % 3]
                    col = b * 16 + blk
                    S.add("pool", lambda e, kg_=kg_, col=col: e.indirect_dma_start(
                        out=kg_, out_offset=None, in_=ck[:, :], in_offset=bass.IndirectOffsetOnAxis(ap=idxall[:, col:col + 1], axis=0)),
                        [idxb], [kgb_], dma=kgb_)
                    S.add("pool", lambda e, vg_=vg_, col=col: e.indirect_dma_start(
                        out=vg_, out_offset=None, in_=cv[:, :], in_offset=bass.IndirectOffsetOnAxis(ap=idxall[:, col:col + 1], axis=0)),
                        [idxb], [vgb_], dma=vgb_)
                    bt = ps_next(2, 8)
                    for j in range(4):
                        A_tr(psbf[bt][:, j * 128:(j + 1) * 128], kg_[:, j * 128:(j + 1) * 128], identb[:], [kgb_, cb], [psb[bt]])
                    kt_, ktb_ = kTbs[it % 2]
                    A_copy(ev_eng(), kt_, psbf[bt][:, 0:512], [psb[bt]], [ktb_])
                    kt3 = kt_.rearrange("p (a s) -> p a s", a=4)
                    vsrc, vb_ = vg_, vgb_
                else:
                    kt3, ktb_ = kTn3, kTb[16]
                    vsrc, vb_ = vnew, vtb[16]
                if SUB < 4:
                    continue
                bq = ps_next(2, 8)
                for kv in range(4):
                    A_mm(ps[bq][:, 32 * kv:32 * (kv + 1)], kt3[:, kv, :], actT[:, 4 * kv:4 * kv + 4, 8 * b:8 * b + 8], True, True,
                         [ktb_] + qb[4 * kv:4 * kv + 4], [psb[bq]])
                e_, eb_ = ets[it % 3]
                A_act(e_, ps[bq][:, 0:128], AF.Exp, [psb[bq]], [eb_], scale=ATT_SCALE)
                p_, pb_ = pts[it % 3]
                A_tt(p_.rearrange("p (a t) -> p a t", a=16), e_.rearrange("p (a t) -> p a t", a=16),
                     sT3[:, blk:blk + 1, 8 * b:8 * b + 8].to_broadcast([128, 16, 8]), ALU.mult, [eb_, sTb], [pb_])
                if SUB < 5:
                    continue
                for kv in range(4):
                    A_mm(ps[bo][:, 32 * kv:32 * (kv + 1)], vsrc[:, kv * 128:(kv + 1) * 128], p_[:, 32 * kv:32 * (kv + 1)],
                         blk == 0, blk == 16, [vb_, pb_], [psb[bo]])
                A_mm(ps[bs_][:, 0:128], onesb[:], p_, blk == 0, blk == 16, [pb_, cb], [psb[bs_]])
            if SUB < 6:
                continue
            A_copy("act", rs, ps[bs_][:, 0:128], [psb[bs_]], [rsb])
            S.add("dve", lambda e: e.reciprocal(out=rs, in_=rs), [rsb], [rsb])
            if SUB < 7:
                continue
            A_tt(R["oS"][:, 128 * b:128 * (b + 1)], ps[bo][:, 0:128], rs, ALU.mult, [psb[bo], rsb], [R["oSb"]])

    for g in PASSES:
        sample = g == 4
        T, NT = (128, 1) if sample else (512, 4)
        load_x(xs if sample else xp, 0 if sample else 512 * g, NT)
        if STAGE >= 2:
            mixer_a(T, NT, sample)
        if STAGE >= 3:
            conv_ffn(0, 1, T, NT, sample, g == 3)
        if STAGE >= 4:
            R = mixer_b(g, T, NT, sample)
        if STAGE >= 5:
            if sample:
                sample_attention(R)
            else:
                prompt_attention(g, T, NT, R)
        if STAGE >= 6:
            if sample:
                oS3 = R["oS"].rearrange("p (b m) -> p b m", b=16)
                proj_b_residual(b_w_o, 16, T, lambda k: oS3[:, :, 8 * k:8 * k + 8], lambda k: [R["oSb"]])
            else:
                proj_b_residual(b_w_o, 16, T, lambda k: actT[:, k, :T], lambda k: act_bufs(k, NT))
            conv_ffn(1, 3, T, NT, sample, g == 3)
        store_y(y_s if sample else y_p, 0 if sample else 512 * g, NT)

    S.emit(final_wait_bufs=out_bufs)
    ts.close()
    return nc


def _consts(inp):
    f = np.float32
    a_norm, f_norm, b_norm = inp["a_norm"], inp["f_norm"], inp["b_norm"]
    g4 = np.stack([a_norm[0], f_norm[0], b_norm[0], f_norm[1]]).astype(f)
    gains = np.ascontiguousarray(g4.reshape(4, 16, 128).transpose(2, 0, 1).reshape(128, 64))
    vg = inp["a_v_norm"][0].astype(f)
    ws = inp["a_w_s"][0].astype(f)
    wsT_p = np.ascontiguousarray(ws.transpose(2, 0, 1).reshape(128, 2048))
    small = ws[:, :8, :8].transpose(2, 0, 1)
    blk = np.zeros((16, 8, 16, 16, 8), f)
    for b in range(16):
        blk[b, :, :, b, :] = small
    wsT_s = np.ascontiguousarray(blk.reshape(128, 2048))
    bs = inp["a_b_s"][0].astype(f)
    bs_p = np.ascontiguousarray(bs.reshape(1, 2048))
    bs_s = np.ascontiguousarray(np.tile(bs[:, :8], (1, 16)).reshape(1, 2048))
    cwl = []
    for l in range(2):
        arr = np.concatenate([inp["f_conv_w"][l], inp["f_conv_b"][l][None]], 0).astype(f)
        cwl.append(arr.reshape(4, 88, 128).transpose(2, 1, 0))
    cw = np.ascontiguousarray(np.stack(cwl, 1).reshape(128, 2 * 88 * 4))
    pos = np.concatenate([np.arange(SEQ), SEQ + (np.arange(128) % 8)]).astype(f)

    def table(d):
        half = d // 2
        inv = (np.float32(10000.0) ** (-(np.arange(half, dtype=f)) * f(2.0 / d))).astype(f)
        ang = (pos[:, None] * inv[None, :]).astype(f)
        return np.ascontiguousarray(np.concatenate([np.cos(ang), np.sin(ang)], 1).astype(f))
    newmask = np.full((128, 128), NEG, f)
    for r in range(128):
        b, t = divmod(r, 8)
        newmask[r, 8 * b:8 * b + t + 1] = 0.0
    ee = np.zeros((128, 248), f)
    for r in range(128):
        ee[r, 120 + (r % 8)] = 1.0
    return dict(
        a_w_in=np.ascontiguousarray(inp["a_w_in"][0]), a_w_out=np.ascontiguousarray(inp["a_w_out"][0]),
        b_w_in=np.ascontiguousarray(inp["b_w_in"][0]), b_w_o=np.ascontiguousarray(inp["b_w_o"][0]),
        f_w_in=np.ascontiguousarray(inp["f_w_in"]), f_w_out=np.ascontiguousarray(inp["f_w_out"]),
        gains=gains, vgT=np.ascontiguousarray(vg.reshape(16, 128).T), vg_row=np.ascontiguousarray(vg[None, :]),
        wsT_p=wsT_p, wsT_s=wsT_s, bs_p=bs_p, bs_s=bs_s,
        qg_row=np.ascontiguousarray(inp["b_q_norm"].astype(f).reshape(1, 128)),
        kg_row=np.ascontiguousarray(inp["b_k_norm"].astype(f).reshape(1, 128)),
        cw=cw, ropeqk=table(128), ropei=table(64), c_newmask=newmask, c_ee=ee)


def _core_inputs(inp, c):
    return dict(
        xp=np.ascontiguousarray(inp["x_prompt"][c]),
        xs=np.ascontiguousarray(inp["x_sample"][16 * c:16 * c + 16].reshape(128, D)),
        conv_s=np.ascontiguousarray(inp["state_ffn_conv"][:, 16 * c:16 * c + 16].reshape(2, 32, 2 * DFF)),
        ptab=np.ascontiguousarray(inp["page_table"][16 * c:16 * c + 16].reshape(1, 256).astype(np.int32)))


OUT_SHAPES = None


def _assemble(results, cores):
    f = np.float32
    nb, nsb = 8, 128
    y_p = np.zeros((nb, SEQ, D), f)
    y_s = np.zeros((nsb, 8, D), f)
    nk_p = np.zeros((1, nb, 16, 128, 4, 128), f)
    nv_p = np.zeros((1, nb, 16, 128, 4, 128), f)
    nik_p = np.zeros((1, nb, 16, 128, 64), f)
    nk_s = np.zeros((1, nsb, 8, 4, 128), f)
    nv_s = np.zeros((1, nsb, 8, 4, 128), f)
    nik_s = np.zeros((1, nsb, 8, 64), f)
    ncv_s = np.zeros((1, nsb, 8, D), f)
    ncp = np.zeros((2, nb, 2, 2 * DFF), f)
    ncs = np.zeros((2, nsb, 2, 2 * DFF), f)
    for r, c in zip(results, cores):
        sl = slice(16 * c, 16 * c + 16)
        y_p[c] = r["y_p"]
        y_s[sl] = r["y_s"].reshape(16, 8, D)
        nk_p[0, c] = r["nk_p"].reshape(16, 128, 4, 128)
        nv_p[0, c] = r["nv_p"].reshape(16, 128, 4, 128)
        nik_p[0, c] = r["nik_p"].reshape(16, 128, 64)
        nk_s[0, sl] = r["nk_s"].reshape(16, 8, 4, 128)
        nv_s[0, sl] = r["nv_s"].reshape(16, 8, 4, 128)
        nik_s[0, sl] = r["nik_s"].reshape(16, 8, 64)
        ncv_s[0, sl] = r["ncv_s"].reshape(16, 8, D)
        ncp[:, c] = r["nconv_p"]
        ncs[:, sl] = r["nconv_s"].reshape(2, 16, 2, 2 * DFF)
    return (y_p, y_s, nk_p, nv_p, nik_p, nk_s, nv_s, nik_s, ncv_s, ncp, ncs)


PASSES = (0, 1, 2, 3, 4)
STAGE = 9
PIPE = True
SUB = 9


def kernel(**inputs):
    inp = {k: np.asarray(v) for k, v in inputs.items()}
    NP = inp["cache_k"].shape[1]
    nc = build_program(NP)
    common = _consts(inp)
    common["ck"] = np.ascontiguousarray(inp["cache_k"][0].reshape(NP * PAGE, 512))
    common["cv"] = np.ascontiguousarray(inp["cache_v"][0].reshape(NP * PAGE, 512))
    common["cik"] = np.ascontiguousarray(inp["cache_idx_k"][0].reshape(NP * PAGE, 64))
    cores = list(range(8))
    in_maps = [dict(common, **_core_inputs(inp, c)) for c in cores]
    res = run_bass_kernel_spmd(nc, in_maps, core_ids=cores)
    return _assemble(res.results, cores)
```
